# Optimizing a Trainium2 kernel written in Bass

```python
import math
import jax, jax.numpy as jnp
from jax import lax
import numpy as np

D_MODEL = 1024
BATCH = 32
SEQ = 2048
DEPTH = 1

GM_HEADS = 8
GM_HEAD_DIM = 128
GM_WIDTH = GM_HEADS * GM_HEAD_DIM
CHUNK = 128
DA_HEADS = 8
DA_QK_DIM = 64
DA_V_DIM = 2 * DA_QK_DIM
DA_WIDTH = DA_HEADS * DA_V_DIM
DA_QK_WIDTH = DA_HEADS * 2 * DA_QK_DIM
ROPE_THETA = 10000.0
Q_BLOCK = 128
EPS = 1e-6

SPLITS = [GM_WIDTH, GM_WIDTH, GM_WIDTH,
          DA_QK_WIDTH, DA_QK_WIDTH, DA_WIDTH, DA_WIDTH,
          D_MODEL, D_MODEL]
D_IN = sum(SPLITS)
SPLIT_POINTS = [int(s) for s in np.cumsum(SPLITS)[:-1]]

kernel_name = "hybrid_gmlp_diffattn_gated_block"


def rmsnorm(x, g):
    xf = x.astype(jnp.float32)
    out = xf * lax.rsqrt(jnp.mean(xf * xf, axis=-1, keepdims=True) + EPS)
    return (out * g.astype(jnp.float32)).astype(x.dtype)


def layernorm(x, g, b):
    xf = x.astype(jnp.float32)
    mu = jnp.mean(xf, axis=-1, keepdims=True)
    var = jnp.mean(jnp.square(xf - mu), axis=-1, keepdims=True)
    out = (xf - mu) * lax.rsqrt(var + EPS) * g.astype(jnp.float32) + b.astype(jnp.float32)
    return out.astype(x.dtype)


def rope(x, cos, sin):
    half = x.shape[-1] // 2
    x1, x2 = x[..., :half], x[..., half:]
    cos = cos.astype(x.dtype)
    sin = sin.astype(x.dtype)
    return jnp.concatenate([x1 * cos - x2 * sin, x2 * cos + x1 * sin], axis=-1)


def gmlp_branch(u, v, ln_g, ln_b, ws, bs):
    B, S, _ = v.shape
    nc = S // CHUNK
    vn = layernorm(v, ln_g, ln_b).reshape(B, nc, CHUNK, GM_HEADS, GM_HEAD_DIM)
    sv = jnp.einsum('hij,bcjhd->bcihd', ws.astype(v.dtype), vn)
    sv = sv + bs.T.astype(v.dtype)[None, None, :, :, None]
    return u * sv.reshape(B, S, GM_WIDTH)


def diff_attention(q, k, v, lq1, lk1, lq2, lk2, subln_g, lam_init, cos, sin):
    B, S, _ = q.shape
    q = q.reshape(B, S, DA_HEADS, 2, DA_QK_DIM)
    k = k.reshape(B, S, DA_HEADS, 2, DA_QK_DIM)
    v = v.reshape(B, S, DA_HEADS, DA_V_DIM)
    scale = DA_QK_DIM ** -0.5
    q1 = rope(q[..., 0, :], cos, sin) * scale
    q2 = rope(q[..., 1, :], cos, sin) * scale
    k1 = rope(k[..., 0, :], cos, sin)
    k2 = rope(k[..., 1, :], cos, sin)
    lam = (jnp.exp(jnp.sum(lq1.astype(jnp.float32) * lk1.astype(jnp.float32)))
           - jnp.exp(jnp.sum(lq2.astype(jnp.float32) * lk2.astype(jnp.float32)))
           + lam_init)
    nb = S // Q_BLOCK
    q1b = q1.reshape(B, nb, Q_BLOCK, DA_HEADS, DA_QK_DIM).transpose(1, 0, 2, 3, 4)
    q2b = q2.reshape(B, nb, Q_BLOCK, DA_HEADS, DA_QK_DIM).transpose(1, 0, 2, 3, 4)

    def block(args):
        a1, a2 = args
        p1 = jax.nn.softmax(jnp.einsum('bqhd,bkhd->bhqk', a1, k1).astype(jnp.float32), axis=-1)
        p2 = jax.nn.softmax(jnp.einsum('bqhd,bkhd->bhqk', a2, k2).astype(jnp.float32), axis=-1)
        attn = (p1 - lam * p2).astype(v.dtype)
        return jnp.einsum('bhqk,bkhe->bqhe', attn, v)

    o = lax.map(block, (q1b, q2b))
    o = o.transpose(1, 0, 2, 3, 4).reshape(B, S, DA_HEADS, DA_V_DIM)
    o = rmsnorm(o, subln_g) * jnp.asarray(1.0 - lam_init, dtype=o.dtype)
    return o.reshape(B, S, DA_WIDTH)


def setup_inputs(seed: int = 0) -> dict:
    key = jax.random.key(seed)
    ks = jax.random.split(key, 16)
    f32 = jnp.float32
    n = lambda k, shape: jax.random.normal(k, shape, dtype=f32)
    return {
        "x": n(ks[0], (BATCH, SEQ, D_MODEL)),
        "ln_pre_g": 1.0 + 0.05 * n(ks[1], (DEPTH, D_MODEL)),
        "w_in": n(ks[2], (DEPTH, D_MODEL, D_IN)) * D_MODEL ** -0.5,
        "gm_ln_g": 1.0 + 0.05 * n(ks[3], (DEPTH, GM_WIDTH)),
        "gm_ln_b": 0.05 * n(ks[4], (DEPTH, GM_WIDTH)),
        "gm_ws": n(ks[5], (DEPTH, GM_HEADS, CHUNK, CHUNK)) * CHUNK ** -0.5,
        "gm_bs": 1.0 + 0.1 * n(ks[6], (DEPTH, GM_HEADS, CHUNK)),
        "lambda_q1": 0.1 * n(ks[7], (DEPTH, DA_QK_DIM)),
        "lambda_k1": 0.1 * n(ks[8], (DEPTH, DA_QK_DIM)),
        "lambda_q2": 0.1 * n(ks[9], (DEPTH, DA_QK_DIM)),
        "lambda_k2": 0.1 * n(ks[10], (DEPTH, DA_QK_DIM)),
        "da_subln_g": 1.0 + 0.05 * n(ks[11], (DEPTH, DA_V_DIM)),
        "w_branch_a": n(ks[12], (DEPTH, GM_WIDTH, D_MODEL)) * GM_WIDTH ** -0.5,
        "w_branch_b": n(ks[13], (DEPTH, DA_WIDTH, D_MODEL)) * DA_WIDTH ** -0.5,
        "w_out": n(ks[14], (DEPTH, D_MODEL, D_MODEL)) * D_MODEL ** -0.5,
        "ln_post_g": 1.0 + 0.05 * n(ks[15], (DEPTH, D_MODEL)),
    }


def reference(x, ln_pre_g, w_in, gm_ln_g, gm_ln_b, gm_ws, gm_bs, lambda_q1, lambda_k1,
              lambda_q2, lambda_k2, da_subln_g, w_branch_a, w_branch_b, w_out, ln_post_g):
    S = x.shape[1]
    pos = jnp.arange(S, dtype=jnp.float32)
    inv_freq = 1.0 / (ROPE_THETA ** (jnp.arange(0, DA_QK_DIM, 2, dtype=jnp.float32) / DA_QK_DIM))
    ang = pos[:, None] * inv_freq[None, :]
    cos = jnp.cos(ang)[:, None, :]
    sin = jnp.sin(ang)[:, None, :]
    for l in range(DEPTH):
        lam_init = 0.8 - 0.6 * math.exp(-0.3 * l)
        h = rmsnorm(x, ln_pre_g[l])
        proj = jnp.einsum('bsd,de->bse', h, w_in[l])
        u, v, za, q, k, vv, zb, ga, gb = jnp.split(proj, SPLIT_POINTS, axis=-1)
        ya = gmlp_branch(u, v, gm_ln_g[l], gm_ln_b[l], gm_ws[l], gm_bs[l]) * jax.nn.silu(za)
        yb = diff_attention(q, k, vv, lambda_q1[l], lambda_k1[l], lambda_q2[l], lambda_k2[l],
                            da_subln_g[l], lam_init, cos, sin) * jax.nn.silu(zb)
        merged = (jax.nn.sigmoid(ga) * jnp.einsum('bse,ed->bsd', ya, w_branch_a[l])
                  + jax.nn.sigmoid(gb) * jnp.einsum('bse,ed->bsd', yb, w_branch_b[l]))
        out = jnp.einsum('bsd,de->bse', merged, w_out[l])
        x = x + rmsnorm(out, ln_post_g[l])
    return x
```

```python
import numpy as np
from contextlib import ExitStack

import concourse.bass as bass
import concourse.mybir as mybir
from concourse.bass_utils import run_bass_kernel_spmd

F32 = mybir.dt.float32
BF16 = mybir.dt.bfloat16
AF = mybir.ActivationFunctionType
ALU = mybir.AluOpType
AX = mybir.AxisListType

NCORES = 8
S = 2048
D = 1024
DC = 8
NTT = 16
EPS = 1e-6
LAM_INIT = 0.2
DIN = 9216
C_U, C_V, C_ZA, C_Q, C_K, C_VV, C_ZB, C_GA, C_GB = [i * 1024 for i in range(9)]


class Gen:
    def __init__(self, nc, stack):
        self.nc = nc
        self.stack = stack
        self.eng = dict(pe=nc.tensor, act=nc.scalar, dve=nc.vector, pool=nc.gpsimd, sp=nc.sync)
        self.sem = {}
        self.cnt = {}
        self.waited = {}
        for e in ["pe", "act", "dve", "pool"]:
            self.newsem(e)

    def newsem(self, name):
        self.sem[name] = self.stack.enter_context(self.nc.semaphore("s_" + name))
        self.cnt[name] = 0

    def wait(self, e, *toks):
        for t in toks:
            if t is None:
                continue
            if isinstance(t, list):
                self.wait(e, *t)
                continue
            name, val = t
            if val <= 0:
                continue
            k = (e, name)
            if self.waited.get(k, 0) >= val:
                continue
            self.waited[k] = val
            self.eng[e].wait_ge(self.sem[name], val)

    def mark(self, e, inst):
        self.cnt[e] += 1
        inst.then_inc(self.sem[e], 1)
        return (e, self.cnt[e])

    def last(self, e):
        return (e, self.cnt[e])

    def dma(self, e, out, in_, semname):
        inst = self.eng[e].dma_start(out=out, in_=in_)
        self.cnt[semname] += 16
        inst.then_inc(self.sem[semname], 16)
        return (semname, self.cnt[semname])

    def barrier(self):
        toks = [self.last(e) for e in ["pe", "act", "dve", "pool"]]
        for e in ["pe", "act", "dve", "pool", "sp"]:
            self.wait(e, *toks)


class _Stop(Exception):
    pass


STOP = [None]
OPT = dict(trdelay=7, g2late=True, mlate=True)


def _chk(stage):
    if STOP[0] == stage:
        raise _Stop()


def build_program(nb):
    nc = bass.Bass("TRN2", target_bir_lowering=False)
    x_d = nc.dram_tensor("x", [nb, S, D], F32, kind="ExternalInput").ap()
    w_in_d = nc.dram_tensor("w_in", [D, DIN], F32, kind="ExternalInput").ap()
    w_a_d = nc.dram_tensor("w_a", [D, D], F32, kind="ExternalInput").ap()
    w_b_d = nc.dram_tensor("w_b", [D, D], F32, kind="ExternalInput").ap()
    w_o_d = nc.dram_tensor("w_o", [D, D], F32, kind="ExternalInput").ap()
    gpreT_d = nc.dram_tensor("gpreT", [128, 8], F32, kind="ExternalInput").ap()
    gmgT_d = nc.dram_tensor("gmgT", [128, 8], F32, kind="ExternalInput").ap()
    gmb_d = nc.dram_tensor("gmb_row", [1, 1024], F32, kind="ExternalInput").ap()
    wsT_d = nc.dram_tensor("wsT", [128, 1024], F32, kind="ExternalInput").ap()
    bs_d = nc.dram_tensor("bs_row", [1, 1024], F32, kind="ExternalInput").ap()
    lams_d = nc.dram_tensor("lams", [1, 256], F32, kind="ExternalInput").ap()
    subg_d = nc.dram_tensor("subg_row", [1, 128], F32, kind="ExternalInput").ap()
    gpost_d = nc.dram_tensor("gpost_row", [1, 1024], F32, kind="ExternalInput").ap()
    ident_d = nc.dram_tensor("ident", [128, 128], F32, kind="ExternalInput").ap()
    rotT_d = nc.dram_tensor("rotT", [128, 128], F32, kind="ExternalInput").ap()
    cos_d = nc.dram_tensor("cosT", [128, S], F32, kind="ExternalInput").ap()
    sin_d = nc.dram_tensor("sinT", [128, S], F32, kind="ExternalInput").ap()
    out_d = nc.dram_tensor("out", [nb, S, D], F32, kind="ExternalOutput").ap()

    w_in_v = w_in_d.rearrange("(dc p) c -> p dc c", p=128)
    w_a_v = w_a_d.rearrange("(dc p) c -> p dc c", p=128)
    w_b_v = w_b_d.rearrange("(dc p) c -> p dc c", p=128)
    w_o_v = w_o_d.rearrange("(dc p) c -> p dc c", p=128)

    A = nc.alloc_sbuf_tensor
    hT = A("hT", [128, 8, S], BF16)
    ybT = A("ybT", [128, 8, S], BF16)
    Wt = [A("W0", [128, 8, 1024], BF16), A("W1", [128, 8, 1024], BF16)]
    cosT = A("cos", [128, S], F32)
    sinT = A("sin", [128, S], F32)
    xin = [A("xin0", [128, 1024], F32), A("xin1", [128, 1024], F32)]
    ybuf = [A("yb0", [128, 1024], F32), A("yb1", [128, 1024], F32)]
    junk = A("junk", [128, 1024], BF16)
    gpost_bc = A("gpost_bc", [128, 1024], F32)
    Cq = A("Cq", [128, 8, 128], F32)
    wsT = A("wsT_bf", [128, 8, 128], BF16)
    ident = A("ident_bf", [128, 128], BF16)
    rotT = A("rotT_bf", [128, 128], BF16)
    gsub_bc = A("gsub_bc", [128, 128], F32)
    gpreT = A("gpreT_sb", [128, 8], F32)
    gq = A("gq", [128, 8], F32)
    neghalf = A("neghalf", [128, 4], F32)
    neg_lam = A("neg_lam", [128, 1], F32)
    lamtmp = A("lamtmp", [128, 8], F32)
    onesrow = A("onesrow", [1, 128], F32)
    NS = nb * NTT
    st_ss = A("st_ss", [128, NS], F32)
    st_ms = A("st_ms", [128, NS], F32)
    st_rs = A("st_rs", [128, NS], F32)
    lv_sum = A("lv_sum", [128, NS], F32)
    lv_sq = A("lv_sq", [128, NS], F32)
    lv_mean = A("lv_mean", [128, NS], F32)
    lv_var = A("lv_var", [128, NS], F32)
    lv_rs = A("lv_rs", [128, NS], F32)
    o_ss = A("o_ss", [128, NS], F32)
    o_ms = A("o_ms", [128, NS], F32)
    o_rs = A("o_rs", [128, NS], F32)
    rd_t = [A(f"rd{i}", [128, 2, 2], F32) for i in range(2)]
    s2_t = [A(f"s2{i}", [128, 2], F32) for i in range(2)]
    t_sb = [A(f"tsb{i}", [128, 2, 128], F32) for i in range(2)]
    u2_sb = [A(f"u2sb{i}", [128, 2, 128], F32) for i in range(2)]
    o_sb = [A(f"osb{i}", [128, 2, 128], F32) for i in range(2)]
    sq_sb = [A(f"sqsb{i}", [128, 2, 128], F32) for i in range(2)]
    ssq_t = [A(f"ssq{i}", [128, 2], F32) for i in range(2)]
    ms_t = [A(f"mst{i}", [128, 2], F32) for i in range(2)]
    rs_t = [A(f"rst{i}", [128, 2], F32) for i in range(2)]
    yb_sb = [A(f"ybsb{i}", [128, 128], BF16) for i in range(8)]
    th_a2 = [A(f"tha2{i}", [128, 256], F32) for i in range(2)]
    RN = 25600
    R = A("R", [128, RN], BF16)
    kq = R[:, 0:8192].rearrange("p (a b t) -> p a b t", a=2, b=2)
    v_aug = R[:, 8192:8192 + 4160].rearrange("p (t h e) -> p t h e", t=16, h=2)
    o1 = 8192 + 4160
    gate = R[:, o1:o1 + 4096].rearrange("p (t e) -> p t e", t=16)
    o2 = o1 + 4096
    E = [R[:, o2 + i * 1024: o2 + (i + 1) * 1024] for i in range(3)]
    o3 = o2 + 3072
    raw = [R[:, o3 + i * 512: o3 + (i + 1) * 512] for i in range(2)]
    o4 = o3 + 1024
    t12 = [[R[:, o4 + (i * 2 + j) * 1024: o4 + (i * 2 + j + 1) * 1024].bitcast(F32) for j in range(2)]
           for i in range(2)]
    assert o4 + 4096 <= RN
    xn = [R[:, 16384 + i * 1024: 16384 + (i + 1) * 1024] for i in range(2)]
    yaT = R[:, 0:8192].rearrange("p (g t) -> p g t", g=8)
    vhat = R[:, 8192:16384].rearrange("p (t e) -> p t e", t=8)
    merged = R[:, 8192:16384].rearrange("p (g t) -> p g t", g=8)
    scr = [[R[:, 16384 + (i * 4 + j) * 1024: 16384 + (i * 4 + j + 1) * 1024].bitcast(F32) for j in range(4)]
           for i in range(2)]
    assert 16384 + 8192 <= RN
    b_bc = R[:, 0:2048].bitcast(F32)
    wsT_f = R[:, 2048:4096].bitcast(F32)
    bs_sb = R[0:1, 4096:6144].bitcast(F32)
    lq_bc = R[:, 6144:6656].bitcast(F32)
    subg_bc = R[:, 6656:6912].bitcast(F32)
    gmgT_sb = R[:, 6912:6928].bitcast(F32)
    lprod = R[:, 7168:7424].bitcast(F32)

    PS = [nc.alloc_psum_tensor(f"ps{i}", [128, 1024], F32) for i in range(4)]
    PSB = [p.bitcast(BF16) for p in PS]

    def bank(i):
        return PS[i // 2][:, (i % 2) * 512:(i % 2 + 1) * 512]

    def bank_bf(i):
        return PSB[i // 2][:, (i % 2) * 1024:(i % 2 + 1) * 1024]

    with ExitStack() as stack:
        g = Gen(nc, stack)
        for nm in ["w0", "w1", "xin0", "xin1", "out0", "out1", "cst", "cstp"]:
            g.newsem(nm)
        pe, act, dve, pool, sp = nc.tensor, nc.scalar, nc.vector, nc.gpsimd, nc.sync
        bank_free = [None] * 8

        def acquire(e, banks):
            g.wait(e, *[bank_free[b] for b in banks])

        def release(banks, tok):
            for b in banks:
                bank_free[b] = tok

        c = []
        c.append(g.dma("sp", cosT[:], cos_d, "cst"))
        c.append(g.dma("sp", sinT[:], sin_d, "cst"))
        c.append(g.dma("sp", gpost_bc[:], gpost_d.partition_broadcast(128), "cst"))
        c.append(g.dma("sp", gpreT[:], gpreT_d, "cst"))
        c.append(g.dma("sp", b_bc, gmb_d.partition_broadcast(128), "cst"))
        c.append(g.dma("sp", wsT_f, wsT_d, "cst"))
        c.append(g.dma("sp", bs_sb, bs_d, "cst"))
        c.append(g.dma("sp", lq_bc, lams_d.partition_broadcast(128), "cst"))
        c.append(g.dma("sp", subg_bc, subg_d.partition_broadcast(128), "cst"))
        c.append(g.dma("sp", gmgT_sb, gmgT_d, "cst"))
        cst_tok = c[-1]
        g.dma("pool", wsT[:].rearrange("p g i -> p (g i)"), wsT_d, "cstp")
        g.dma("pool", ident[:], ident_d, "cstp")
        cstp_tok = g.dma("pool", rotT[:], rotT_d, "cstp")
        for e in ["pe", "act", "dve", "pool"]:
            g.wait(e, cst_tok, cstp_tok)
        pool.memset(neghalf[:], -0.5)
        pool.memset(onesrow[:], 1.0)
        for t in [st_ss, lv_sum, lv_sq, o_ss]:
            pool.memset(t[:], 0.0)
        tk_pool0 = g.mark("pool", pool.memset(junk[:], 0.0))
        dve.tensor_tensor(out=lprod[:, 0:64], in0=lq_bc[:, 0:64], in1=lq_bc[:, 64:128], op=ALU.mult)
        tk = g.mark("dve", dve.tensor_tensor(out=lprod[:, 64:128], in0=lq_bc[:, 128:192],
                                             in1=lq_bc[:, 192:256], op=ALU.mult))
        g.wait("dve", tk)
        dve.tensor_reduce(out=lamtmp[:, 0:1], in_=lprod[:, 0:64], axis=AX.X, op=ALU.add)
        tk = g.mark("dve", dve.tensor_reduce(out=lamtmp[:, 1:2], in_=lprod[:, 64:128], axis=AX.X, op=ALU.add))
        g.wait("act", tk)
        tk = g.mark("act", act.activation(out=lamtmp[:, 2:4], in_=lamtmp[:, 0:2], func=AF.Exp))
        g.wait("dve", tk)
        tk = g.mark("dve", dve.tensor_tensor(out=lamtmp[:, 4:5], in0=lamtmp[:, 3:4], in1=lamtmp[:, 2:3],
                                             op=ALU.subtract))
        g.wait("dve", tk)
        tk = g.mark("dve", dve.tensor_scalar(out=neg_lam[:], in0=lamtmp[:, 4:5], scalar1=-LAM_INIT,
                                             scalar2=None, op0=ALU.add))
        dve.tensor_scalar(out=gsub_bc[:], in0=subg_bc, scalar1=(1.0 - LAM_INIT) * 0.25, scalar2=None,
                          op0=ALU.mult)
        tk = g.mark("dve", dve.tensor_scalar(out=gq[:], in0=gmgT_sb, scalar1=0.25, scalar2=None, op0=ALU.mult))
        g.wait("pe", tk_pool0)
        for gg in range(8):
            o = PS[gg // 4][:, (gg % 4) * 128:(gg % 4 + 1) * 128]
            pe.matmul(o, lhsT=b_bc[:, gg * 128:(gg + 1) * 128], rhs=wsT_f[:, gg * 128:(gg + 1) * 128],
                      start=True, stop=False)
            ins = pe.matmul(o, lhsT=onesrow[0:1, :], rhs=bs_sb[0:1, gg * 128:(gg + 1) * 128],
                            start=False, stop=True)
        tk = g.mark("pe", ins)
        g.wait("dve", tk)
        dve.tensor_scalar(out=Cq[:, 0:4, :].rearrange("p g i -> p (g i)"), in0=PS[0][:, 0:512], scalar1=0.25,
                          scalar2=None, op0=ALU.mult)
        tk = g.mark("dve", dve.tensor_scalar(out=Cq[:, 4:8, :].rearrange("p g i -> p (g i)"), in0=PS[1][:, 0:512],
                                             scalar1=0.25, scalar2=None, op0=ALU.mult))
        g.barrier()

        wstate = dict(idx=0, free=[None, None], ready={})

        def wblock_dmas(kind, arg):
            if kind == "att":
                hp = arg
                return [(j * 256, 256, w_in_v, base + hp * 256) for j, base in enumerate([C_Q, C_K, C_VV, C_ZB])]
            if kind == "g1":
                return [(0, 1024, w_in_v, C_V)]
            if kind == "g2":
                gqi = arg
                return [(0, 512, w_in_v, C_U + gqi * 512), (512, 512, w_in_v, C_ZA + gqi * 512)]
            if kind == "m":
                dp = arg
                return [(0, 256, w_a_v, dp * 256), (256, 256, w_b_v, dp * 256),
                        (512, 256, w_in_v, C_GA + dp * 256), (768, 256, w_in_v, C_GB + dp * 256)]
            if kind == "o":
                return [(0, 1024, w_o_v, 0)]
            raise ValueError(kind)

        blocks = []
        for b in range(nb):
            for hp in range(4):
                blocks.append(("att", hp))
            for hs in range(2):
                blocks.append(("g1", hs))
                blocks.append(("g2", 0))
                blocks.append(("g2", 1))
                for dp in range(4):
                    blocks.append(("m", dp))
                blocks.append(("o", hs))
        wload_i = [0]

        def wload_next():
            i = wload_i[0]
            if i >= len(blocks):
                return
            wload_i[0] += 1
            slot = i % 2
            g.wait("pool", wstate["free"][slot])
            kind, arg = blocks[i]
            tok = None
            for (c0, ncol, src, s0) in wblock_dmas(kind, arg):
                tok = g.dma("pool", Wt[slot][:, :, c0:c0 + ncol], src[:, :, s0:s0 + ncol], "w%d" % slot)
            wstate["ready"][i] = tok

        wuse_i = [0]

        def wuse(kind):
            i = wuse_i[0]
            wuse_i[0] += 1
            assert blocks[i][0] == kind, (blocks[i], kind)
            slot = i % 2
            return Wt[slot], wstate["ready"][i], slot

        def wdone(slot, tok):
            wstate["free"][slot] = tok
            wload_next()

        wload_next()
        wload_next()

        junk_tok = [None]
        hT_war = [None]
        ybT_war = [None]
        xin_free = [None, None]
        ybuf_free = [None, None]
        out_toks = []

        def phase_H(b):
            xn_free = [None, None]
            pend_load = {}

            def load(tt):
                s = tt % 2
                g.wait("sp", xin_free[s])
                pend_load[tt] = g.dma("sp", xin[s][:], x_d[b, tt * 128:(tt + 1) * 128, :], "xin%d" % s)

            def stage1(tt):
                s = tt % 2
                col = b * NTT + tt
                g.wait("act", pend_load[tt], junk_tok[0])
                tA = g.mark("act", act.activation(out=junk[:], in_=xin[s][:], func=AF.Square,
                                                  accum_out=st_ss[:, col:col + 1]))
                junk_tok[0] = tA
                g.wait("dve", tA)
                tD1 = g.mark("dve", dve.tensor_scalar(out=st_ms[:, col:col + 1], in0=st_ss[:, col:col + 1],
                                                      scalar1=1.0 / D, scalar2=EPS, op0=ALU.mult, op1=ALU.add))
                g.wait("pool", tD1)
                tP = g.mark("pool", pool.tensor_tensor(out=st_rs[:, col:col + 1], in0=st_ms[:, col:col + 1],
                                                       in1=neghalf[:, 0:1], op=ALU.pow))
                g.wait("dve", tP, xn_free[s])
                tD2 = g.mark("dve", dve.tensor_scalar(out=xn[s], in0=xin[s][:], scalar1=st_rs[:, col:col + 1],
                                                      scalar2=None, op0=ALU.mult))
                xin_free[s] = tD2
                pb = 2 + (tt % 2)
                acquire("pe", [2 * pb, 2 * pb + 1])
                g.wait("pe", tD2)
                for dc in range(DC):
                    ins = pe.matmul(PS[pb][:, dc * 128:(dc + 1) * 128], lhsT=xn[s][:, dc * 128:(dc + 1) * 128],
                                    rhs=ident[:], start=True, stop=True)
                tT = g.mark("pe", ins)
                xn_free[s] = tT
                return tT, pb

            def stage2(tt, tT, pb):
                g.wait("dve", tT, hT_war[0])
                tD3 = g.mark("dve", dve.tensor_tensor(
                    out=hT[:, :, tt * 128:(tt + 1) * 128],
                    in0=PS[pb][:].rearrange("p (c t) -> p c t", c=8),
                    in1=gpreT[:, :].unsqueeze(2).broadcast_to([128, 8, 128]), op=ALU.mult))
                release([2 * pb, 2 * pb + 1], tD3)

            load(0)
            prev = None
            for tt in range(NTT):
                if tt + 1 < NTT:
                    load(tt + 1)
                cur = stage1(tt)
                if prev is not None:
                    stage2(tt - 1, *prev)
                prev = cur
            stage2(NTT - 1, *prev)
            return g.last("dve")

        def phase_A(b, hT_ready):
            g.wait("pool", g.last("pe"))
            tk_ones = g.mark("pool", pool.memset(v_aug[:, :, :, 128:130], 1.0))
            kq_war = [None]
            vaug_war = [None]
            gate_war = [None]
            E_free = [None, None, None]
            raw_free = [None, None]
            t12_free = [None, None]
            th_free = [None, None]
            post_free = [None, None]
            yb_free = [None] * 8
            ectr = [0]
            sctr = [0]
            uctr = [0]
            gstep = [0]
            pend_tr = []
            for hp in range(4):
                Wb, wtok, wslot = wuse("att")
                g.wait("pe", wtok, hT_ready)
                items = [(which, hh, tc) for which in range(2) for hh in range(2) for tc in range(4)]
                pairs = [(6, 7), (0, 1), (2, 3)]
                pend = None
                kq_ready = None

                def a1_second(it, i, bx, by, tX):
                    which, hh, tc = it
                    r = i % 2
                    g.wait("act", tX, raw_free[r])
                    tA = g.mark("act", act.activation(out=raw[r], in_=bank(bx), func=AF.Copy))
                    acquire("pe", [by])
                    g.wait("pe", tA)
                    tY = g.mark("pe", pe.matmul(bank(by), lhsT=rotT[:], rhs=raw[r], start=True, stop=True))
                    raw_free[r] = tY
                    g.wait("dve", tY, t12_free[r])
                    dve.tensor_tensor(out=t12[r][0], in0=bank(bx), in1=cosT[:, tc * 512:(tc + 1) * 512], op=ALU.mult)
                    tD = g.mark("dve", dve.tensor_tensor(out=t12[r][1], in0=bank(by),
                                                         in1=sinT[:, tc * 512:(tc + 1) * 512], op=ALU.mult))
                    release([bx, by], tD)
                    g.wait("pool", tD, kq_war[0])
                    tP = g.mark("pool", pool.tensor_tensor(out=kq[:, which, hh, tc * 512:(tc + 1) * 512],
                                                           in0=t12[r][0], in1=t12[r][1], op=ALU.add))
                    t12_free[r] = tP
                    return tP

                for i, it in enumerate(items):
                    which, hh, tc = it
                    bx, by = pairs[i % 3]
                    acquire("pe", [bx])
                    wc = (1 - which) * 256 + hh * 128
                    for dc in range(DC):
                        ins = pe.matmul(bank(bx), lhsT=Wb[:, dc, wc:wc + 128], rhs=hT[:, dc, tc * 512:(tc + 1) * 512],
                                        start=(dc == 0), stop=(dc == DC - 1))
                    tX = g.mark("pe", ins)
                    if pend is not None:
                        kq_ready = a1_second(*pend)
                    pend = (it, i, bx, by, tX)
                kq_ready = a1_second(*pend)
                _chk("A1")
                a2banks = [6, 7, 0, 1, 2, 3]
                for tt in range(NTT):
                    bk = a2banks[tt % 6]
                    acquire("pe", [bk])
                    for dc in range(DC):
                        ins = pe.matmul(bank(bk), lhsT=hT[:, dc, tt * 128:(tt + 1) * 128], rhs=Wb[:, dc, 512:1024],
                                        start=(dc == 0), stop=(dc == DC - 1))
                    tX = g.mark("pe", ins)
                    r = tt % 2
                    g.wait("act", tX, th_free[r])
                    tA = g.mark("act", act.activation(out=th_a2[r][:], in_=bank(bk)[:, 256:512], func=AF.Tanh, scale=0.5))
                    g.wait("dve", tA, vaug_war[0], gate_war[0], tk_ones)
                    dve.tensor_copy(out=v_aug[:, tt, :, 0:128],
                                    in_=bank(bk)[:, 0:256].rearrange("p (h e) -> p h e", h=2))
                    tD = g.mark("dve", dve.scalar_tensor_tensor(out=gate[:, tt, :], in0=th_a2[r][:], scalar=1.0,
                                                                in1=bank(bk)[:, 256:512], op0=ALU.add, op1=ALU.mult))
                    th_free[r] = tD
                    release([bk], tD)
                a2_ready = g.last("dve")
                wdone(wslot, g.last("pe"))
                _chk("A2")
                steps = [(hh, cq, st) for hh in range(2) for cq in range(16) for st in range(4)]
                sbanks = [(0, 1), (2, 3)]
                qk_tok = {}

                def emit_qk(si):
                    hh, cq, st = steps[si]
                    sb = sctr[0] % 2
                    sctr[0] += 1
                    acquire("pe", sbanks[sb])
                    g.wait("pe", kq_ready)
                    Sb = PS[sb]
                    for kk in range(4):
                        kt = 4 * st + kk
                        for m in range(2):
                            ins = pe.matmul(Sb[:, (m * 4 + kk) * 128:(m * 4 + kk + 1) * 128],
                                            lhsT=kq[64 * m:64 * m + 64, 0, hh, kt * 128:(kt + 1) * 128],
                                            rhs=kq[64 * m:64 * m + 64, 1, hh, cq * 128:(cq + 1) * 128],
                                            start=True, stop=True)
                    tQ = g.mark("pe", ins)
                    eb = ectr[0] % 3
                    ectr[0] += 1
                    g.wait("act", tQ, E_free[eb])
                    tE = g.mark("act", act.activation(out=E[eb], in_=Sb[:], func=AF.Exp, scale=0.125))
                    release(sbanks[sb], tE)
                    qk_tok[si] = (tE, eb)

                def emit_pv(si):
                    hh, cq, st = steps[si]
                    tE, eb = qk_tok.pop(si)
                    g.wait("pe", tE, a2_ready)
                    u = uctr[0]
                    ob = 4 + (u % 2)
                    if st == 0:
                        acquire("pe", [ob])
                    Ov = bank(ob).rearrange("p (m c) -> p m c", m=2)
                    for kk in range(4):
                        kt = 4 * st + kk
                        for m in range(2):
                            c0 = (m * 4 + kk) * 128
                            ins = pe.matmul(Ov[:, m, 0:129], lhsT=E[eb][:, c0:c0 + 128],
                                            rhs=v_aug[:, kt, hh, 0:129],
                                            start=(st == 0 and kk == 0 and m == 0),
                                            stop=(st == 3 and kk == 3), skip_group_check=True)
                    tP = g.mark("pe", ins)
                    E_free[eb] = tP
                    if st == 3:
                        uctr[0] += 1
                        post(hh, cq, tP, u, ob, si)

                def post(hh, cq, tPV, u, ob, si):
                    s = u % 2
                    Ov = bank(ob).rearrange("p (m c) -> p m c", m=2)
                    rd = rd_t[s][:, 0, :]
                    g.wait("dve", tPV)
                    t0 = g.mark("dve", dve.reciprocal(out=rd, in_=Ov[:, :, 128]))
                    g.wait("dve", t0)
                    dve.tensor_scalar(out=t_sb[s][:, 0, :], in0=Ov[:, 0, 0:128], scalar1=rd[:, 0:1], scalar2=None,
                                      op0=ALU.mult)
                    t1 = g.mark("dve", dve.tensor_scalar(out=s2_t[s][:, 0:1], in0=rd[:, 1:2], scalar1=neg_lam[:, 0:1],
                                                         scalar2=None, op0=ALU.mult))
                    g.wait("dve", t1)
                    tD = g.mark("dve", dve.scalar_tensor_tensor(out=o_sb[s][:, 0, :], in0=Ov[:, 1, 0:128],
                                                                scalar=s2_t[s][:, 0:1], in1=t_sb[s][:, 0, :],
                                                                op0=ALU.mult, op1=ALU.add))
                    release([ob], tD)
                    g.wait("dve", tD)
                    td = g.mark("dve", dve.scalar_tensor_tensor(out=sq_sb[s][:, 0, :], in0=o_sb[s][:, 0, :], scalar=1.0,
                                                                in1=o_sb[s][:, 0, :], op0=ALU.mult, op1=ALU.mult,
                                                                accum_out=ssq_t[s][:, 0:1]))
                    g.wait("dve", td)
                    td = g.mark("dve", dve.tensor_scalar(out=ms_t[s][:, 0:1], in0=ssq_t[s][:, 0:1], scalar1=1.0 / 128,
                                                         scalar2=EPS, op0=ALU.mult, op1=ALU.add))
                    g.wait("pool", td)
                    tp = g.mark("pool", pool.tensor_tensor(out=rs_t[s][:, 0:1], in0=ms_t[s][:, 0:1], in1=neghalf[:, 0:1],
                                                           op=ALU.pow))
                    g.wait("dve", tp)
                    td = g.mark("dve", dve.scalar_tensor_tensor(out=sq_sb[s][:, 1, :], in0=o_sb[s][:, 0, :],
                                                                scalar=rs_t[s][:, 0:1], in1=gsub_bc[:, :],
                                                                op0=ALU.mult, op1=ALU.mult))
                    ys = u % 8
                    g.wait("dve", td, yb_free[ys])
                    tY = g.mark("dve", dve.tensor_tensor(out=yb_sb[ys][:, :], in0=sq_sb[s][:, 1, :],
                                                         in1=gate[:, cq, hh * 128:(hh + 1) * 128], op=ALU.mult))
                    pend_tr.append((hp * 2 + hh, cq, ys, tY, gstep[0]))

                def emit_tr():
                    (h, cq0, ys0, tY0, _), (h1, cq1, ys1, tY1, _) = pend_tr.pop(0), pend_tr.pop(0)
                    assert h == h1 and cq1 == cq0 + 1
                    bk = 6 + ((cq0 // 2) % 2)
                    acquire("pe", [bk])
                    g.wait("pe", tY0, tY1)
                    pe.transpose(bank_bf(bk)[:, 0:128], yb_sb[ys0][:, :], ident[:])
                    tT = g.mark("pe", pe.transpose(bank_bf(bk)[:, 128:256], yb_sb[ys1][:, :], ident[:]))
                    yb_free[ys0] = tT
                    yb_free[ys1] = tT
                    g.wait("dve", tT, ybT_war[0])
                    tD = g.mark("dve", dve.tensor_copy(out=ybT[:, h, cq0 * 128:(cq0 + 2) * 128], in_=bank_bf(bk)[:, 0:256]))
                    release([bk], tD)

                n = len(steps)
                emit_qk(0)
                emit_qk(1)
                for i in range(n):
                    if i + 2 < n:
                        emit_qk(i + 2)
                    emit_pv(i)
                    gstep[0] += 1
                    if len(pend_tr) >= 2 and gstep[0] >= pend_tr[1][4] + OPT['trdelay']:
                        emit_tr()
                if hp == 3:
                    while len(pend_tr) >= 2:
                        emit_tr()
                    assert not pend_tr
                tk_att = g.last("pe")
                kq_war[0] = tk_att
                vaug_war[0] = tk_att
                gate_war[0] = g.last("dve")

        def phase_G(b, hs):
            tok0 = hs * 1024
            Wb, wtok, wslot = wuse("g1")
            g.wait("pe", wtok)
            for tt in range(8):
                gt = hs * 8 + tt
                col = b * NTT + gt
                pb = tt % 4
                acquire("pe", [2 * pb, 2 * pb + 1])
                for half in range(2):
                    for dc in range(DC):
                        ins = pe.matmul(PS[pb][:, half * 512:(half + 1) * 512], lhsT=hT[:, dc, gt * 128:(gt + 1) * 128],
                                        rhs=Wb[:, dc, half * 512:(half + 1) * 512], start=(dc == 0), stop=(dc == DC - 1))
                tX = g.mark("pe", ins)
                g.wait("act", tX, junk_tok[0])
                tA0 = g.mark("act", act.activation(out=junk[:], in_=PS[pb][:], func=AF.Identity,
                                                   accum_out=lv_sum[:, col:col + 1]))
                g.wait("act", tA0)
                tA = g.mark("act", act.activation(out=junk[:], in_=PS[pb][:], func=AF.Square,
                                                  accum_out=lv_sq[:, col:col + 1]))
                junk_tok[0] = tA
                g.wait("dve", tA)
                td = g.mark("dve", dve.tensor_scalar(out=lv_mean[:, col:col + 1], in0=lv_sum[:, col:col + 1],
                                                     scalar1=1.0 / D, scalar2=None, op0=ALU.mult))
                g.wait("dve", td)
                td = g.mark("dve", dve.scalar_tensor_tensor(out=lv_var[:, col:col + 1], in0=lv_mean[:, col:col + 1],
                                                            scalar=-1.0, in1=lv_mean[:, col:col + 1],
                                                            op0=ALU.mult, op1=ALU.mult))
                g.wait("dve", td)
                td = g.mark("dve", dve.scalar_tensor_tensor(out=lv_var[:, col:col + 1], in0=lv_sq[:, col:col + 1],
                                                            scalar=1.0 / D, in1=lv_var[:, col:col + 1],
                                                            op0=ALU.mult, op1=ALU.add))
                g.wait("dve", td)
                td = g.mark("dve", dve.tensor_scalar(out=lv_var[:, col:col + 1], in0=lv_var[:, col:col + 1],
                                                     scalar1=EPS, scalar2=None, op0=ALU.add))
                g.wait("pool", td)
                tp = g.mark("pool", pool.tensor_tensor(out=lv_rs[:, col:col + 1], in0=lv_var[:, col:col + 1],
                                                       in1=neghalf[:, 0:1], op=ALU.pow))
                g.wait("dve", tp)
                tD = g.mark("dve", dve.tensor_scalar(out=vhat[:, tt, :], in0=PS[pb][:], scalar1=lv_mean[:, col:col + 1],
                                                     scalar2=lv_rs[:, col:col + 1], op0=ALU.subtract, op1=ALU.mult))
                release([2 * pb, 2 * pb + 1], tD)
            vhat_ready = g.last("dve")
            wdone(wslot, g.last("pe"))
            _chk("G1")
            scr_free = [None, None]
            ictr = 0
            for gqi in range(2):
                Wb, wtok, wslot = wuse("g2")
                g.wait("pe", wtok)
                if not OPT['g2late']:
                    g.wait("pe", vhat_ready)
                for g4 in range(4):
                    gh = gqi * 4 + g4
                    for tc in range(2):
                        si = ictr % 2
                        ictr += 1
                        bu, bz, bv = (0, 1, 2) if si == 0 else (3, 4, 5)
                        acquire("pe", [bu, bz, bv])
                        t0 = tok0 + tc * 512
                        for dc in range(DC):
                            pe.matmul(bank(bu), lhsT=Wb[:, dc, g4 * 128:(g4 + 1) * 128], rhs=hT[:, dc, t0:t0 + 512],
                                      start=(dc == 0), stop=(dc == DC - 1))
                        for dc in range(DC):
                            pe.matmul(bank(bz), lhsT=Wb[:, dc, 512 + g4 * 128:512 + (g4 + 1) * 128],
                                      rhs=hT[:, dc, t0:t0 + 512], start=(dc == 0), stop=(dc == DC - 1))
                        g.wait("pe", vhat_ready)
                        for cc in range(4):
                            ins = pe.matmul(bank(bv)[:, cc * 128:(cc + 1) * 128],
                                            lhsT=vhat[:, tc * 4 + cc, gh * 128:(gh + 1) * 128], rhs=wsT[:, gh, :],
                                            start=True, stop=True)
                        tX = g.mark("pe", ins)
                        th, sg, tt_, svp = scr[si]
                        g.wait("act", tX, scr_free[si])
                        tA = g.mark("act", act.activation(out=th, in_=bank(bz), func=AF.Tanh, scale=0.5))
                        g.wait("dve", tA)
                        td = g.mark("dve", dve.scalar_tensor_tensor(out=sg, in0=th, scalar=1.0, in1=bank(bz),
                                                                    op0=ALU.add, op1=ALU.mult))
                        g.wait("dve", td)
                        dve.tensor_tensor(out=tt_, in0=bank(bu), in1=sg, op=ALU.mult)
                        tD = g.mark("dve", dve.scalar_tensor_tensor(
                            out=svp.rearrange("p (c i) -> p c i", c=4),
                            in0=bank(bv).rearrange("p (c i) -> p c i", c=4), scalar=gq[:, gh:gh + 1],
                            in1=Cq[:, gh, :].unsqueeze(1).broadcast_to([128, 4, 128]),
                            op0=ALU.mult, op1=ALU.add))
                        release([bu, bz, bv], tD)
                        g.wait("pool", tD)
                        tP = g.mark("pool", pool.tensor_tensor(out=yaT[:, gh, tc * 512:(tc + 1) * 512], in0=svp, in1=tt_,
                                                               op=ALU.mult))
                        scr_free[si] = tP
                wdone(wslot, g.last("pe"))
            ya_ready = g.last("pool")
            _chk("G2")
            for dp in range(4):
                Wb, wtok, wslot = wuse("m")
                g.wait("pe", wtok)
                if not OPT['mlate']:
                    g.wait("pe", ya_ready)
                for dd in range(2):
                    dt_ = dp * 2 + dd
                    for tc in range(2):
                        si = ictr % 2
                        ictr += 1
                        bks = [0, 1, 2, 3] if si == 0 else [4, 5, 6, 7]
                        acquire("pe", bks)
                        t0 = tok0 + tc * 512
                        for dc in range(DC):
                            pe.matmul(bank(bks[1]), lhsT=Wb[:, dc, 256 + dd * 128:256 + (dd + 1) * 128],
                                      rhs=ybT[:, dc, t0:t0 + 512], start=(dc == 0), stop=(dc == DC - 1))
                        for dc in range(DC):
                            pe.matmul(bank(bks[2]), lhsT=Wb[:, dc, 512 + dd * 128:512 + (dd + 1) * 128],
                                      rhs=hT[:, dc, t0:t0 + 512], start=(dc == 0), stop=(dc == DC - 1))
                        for dc in range(DC):
                            pe.matmul(bank(bks[3]), lhsT=Wb[:, dc, 768 + dd * 128:768 + (dd + 1) * 128],
                                      rhs=hT[:, dc, t0:t0 + 512], start=(dc == 0), stop=(dc == DC - 1))
                        g.wait("pe", ya_ready)
                        for dc in range(DC):
                            ins = pe.matmul(bank(bks[0]), lhsT=Wb[:, dc, dd * 128:(dd + 1) * 128],
                                            rhs=yaT[:, dc, tc * 512:(tc + 1) * 512], start=(dc == 0), stop=(dc == DC - 1))
                        tX = g.mark("pe", ins)
                        tha, thb, m1, m2 = scr[si]
                        g.wait("act", tX, scr_free[si])
                        act.activation(out=tha, in_=bank(bks[2]), func=AF.Tanh, scale=0.5)
                        tA = g.mark("act", act.activation(out=thb, in_=bank(bks[3]), func=AF.Tanh, scale=0.5))
                        g.wait("dve", tA)
                        dve.scalar_tensor_tensor(out=m1, in0=tha, scalar=1.0, in1=bank(bks[0]), op0=ALU.add, op1=ALU.mult)
                        tD = g.mark("dve", dve.scalar_tensor_tensor(out=m2, in0=thb, scalar=1.0, in1=bank(bks[1]),
                                                                    op0=ALU.add, op1=ALU.mult))
                        release(bks, tD)
                        g.wait("pool", tD)
                        tP = g.mark("pool", pool.tensor_tensor(out=merged[:, dt_, tc * 512:(tc + 1) * 512], in0=m1, in1=m2,
                                                               op=ALU.add))
                        scr_free[si] = tP
                wdone(wslot, g.last("pe"))
            mg_ready = g.last("pool")
            _chk("M")
            Wb, wtok, wslot = wuse("o")
            g.wait("pe", wtok, mg_ready)
            pend_load = {}

            def loadx(tt):
                s = tt % 2
                gt = hs * 8 + tt
                g.wait("sp", xin_free[s])
                pend_load[tt] = g.dma("sp", xin[s][:], x_d[b, gt * 128:(gt + 1) * 128, :], "xin%d" % s)

            loadx(0)
            for tt in range(8):
                if tt + 1 < 8:
                    loadx(tt + 1)
                gt = hs * 8 + tt
                col = b * NTT + gt
                s = tt % 2
                pb = tt % 4
                acquire("pe", [2 * pb, 2 * pb + 1])
                for half in range(2):
                    for dc in range(DC):
                        ins = pe.matmul(PS[pb][:, half * 512:(half + 1) * 512], lhsT=merged[:, dc, tt * 128:(tt + 1) * 128],
                                        rhs=Wb[:, dc, half * 512:(half + 1) * 512], start=(dc == 0), stop=(dc == DC - 1))
                tX = g.mark("pe", ins)
                g.wait("act", tX, junk_tok[0])
                tA = g.mark("act", act.activation(out=junk[:], in_=PS[pb][:], func=AF.Square,
                                                  accum_out=o_ss[:, col:col + 1]))
                junk_tok[0] = tA
                g.wait("dve", tA)
                td = g.mark("dve", dve.tensor_scalar(out=o_ms[:, col:col + 1], in0=o_ss[:, col:col + 1],
                                                     scalar1=1.0 / D, scalar2=EPS, op0=ALU.mult, op1=ALU.add))
                g.wait("pool", td)
                tp = g.mark("pool", pool.tensor_tensor(out=o_rs[:, col:col + 1], in0=o_ms[:, col:col + 1],
                                                       in1=neghalf[:, 0:1], op=ALU.pow))
                g.wait("dve", tp, ybuf_free[s])
                tD = g.mark("dve", dve.scalar_tensor_tensor(out=ybuf[s][:], in0=PS[pb][:], scalar=o_rs[:, col:col + 1],
                                                            in1=gpost_bc[:], op0=ALU.mult, op1=ALU.mult))
                release([2 * pb, 2 * pb + 1], tD)
                g.wait("dve", tD, pend_load[tt])
                tP = g.mark("dve", dve.tensor_tensor(out=ybuf[s][:], in0=ybuf[s][:], in1=xin[s][:], op=ALU.add))
                xin_free[s] = tP
                g.wait("sp", tP)
                tO = g.dma("sp", out_d[b, gt * 128:(gt + 1) * 128, :], ybuf[s][:], "out%d" % s)
                ybuf_free[s] = tO
                out_toks.append(tO)
            wdone(wslot, g.last("pe"))

        try:
            _chk("setup")
            for b in range(nb):
                hT_ready = phase_H(b)
                _chk("H")
                phase_A(b, hT_ready)
                _chk("A")
                g.barrier()
                for hs in range(2):
                    phase_G(b, hs)
                hT_war[0] = g.last("pe")
                ybT_war[0] = g.last("pe")
                g.barrier()
        except _Stop:
            g.barrier()
        sp.wait_ge(g.sem["out0"], g.cnt["out0"])
        sp.wait_ge(g.sem["out1"], g.cnt["out1"])
    return nc


def _host_consts():
    inv_freq = (1.0 / (10000.0 ** (np.arange(0, 64, 2, dtype=np.float32) / np.float32(64)))).astype(np.float32)
    pos = np.arange(S, dtype=np.float32)
    ang = (pos[:, None] * inv_freq[None, :]).astype(np.float32)
    cos = np.cos(ang).astype(np.float32).T
    sin = np.sin(ang).astype(np.float32).T
    cosT = np.zeros((128, S), np.float32)
    sinT = np.zeros((128, S), np.float32)
    rotT = np.zeros((128, 128), np.float32)
    for p in range(128):
        i = p % 32
        cosT[p] = cos[i]
        if (p % 64) < 32:
            sinT[p] = -sin[i]
            rotT[p + 32, p] = 1.0
        else:
            sinT[p] = sin[i]
            rotT[p - 32, p] = 1.0
    ident = np.eye(128, dtype=np.float32)
    return cosT, sinT, rotT, ident


def make_in_maps(inputs, nb, ncores):
    f = lambda a: np.ascontiguousarray(np.asarray(a, dtype=np.float32))
    x = f(inputs["x"])
    cosT, sinT, rotT, ident = _host_consts()
    shared = {
        "w_in": f(inputs["w_in"][0]),
        "w_a": f(inputs["w_branch_a"][0]),
        "w_b": f(inputs["w_branch_b"][0]),
        "w_o": f(inputs["w_out"][0]),
        "gpreT": f(np.asarray(inputs["ln_pre_g"][0]).reshape(8, 128).T),
        "gmgT": f(np.asarray(inputs["gm_ln_g"][0]).reshape(8, 128).T),
        "gmb_row": f(np.asarray(inputs["gm_ln_b"][0]).reshape(1, 1024)),
        "wsT": f(np.transpose(np.asarray(inputs["gm_ws"][0]), (2, 0, 1)).reshape(128, 1024)),
        "bs_row": f(np.asarray(inputs["gm_bs"][0]).reshape(1, 1024)),
        "lams": f(np.concatenate([np.asarray(inputs[k][0]).reshape(-1) for k in
                                  ["lambda_q1", "lambda_k1", "lambda_q2", "lambda_k2"]]).reshape(1, 256)),
        "subg_row": f(np.asarray(inputs["da_subln_g"][0]).reshape(1, 128)),
        "gpost_row": f(np.asarray(inputs["ln_post_g"][0]).reshape(1, 1024)),
        "ident": ident, "rotT": rotT, "cosT": cosT, "sinT": sinT,
    }
    maps = []
    for c in range(ncores):
        m = dict(shared)
        m["x"] = np.ascontiguousarray(x[c * nb:(c + 1) * nb])
        maps.append(m)
    return maps


_NC_CACHE = {}


def kernel(**inputs):
    nb = inputs["x"].shape[0] // NCORES
    if nb not in _NC_CACHE:
        _NC_CACHE[nb] = build_program(nb)
    nc = _NC_CACHE[nb]
    maps = make_in_maps(inputs, nb, NCORES)
    res = run_bass_kernel_spmd(nc, maps, core_ids=list(range(NCORES)))
    out = np.concatenate([np.asarray(r["out"]) for r in res.results], axis=0)
    return out.astype(np.float32)
```

```python
import numpy as np
from contextlib import ExitStack

import concourse.bass as bass
import concourse.mybir as mybir
from concourse.bass_utils import run_bass_kernel_spmd

F32 = mybir.dt.float32
BF16 = mybir.dt.bfloat16
AF = mybir.ActivationFunctionType
ALU = mybir.AluOpType
AX = mybir.AxisListType

NCORES = 8
S = 2048
D = 1024
DC = 8
NTT = 16
EPS = 1e-6
LAM_INIT = 0.2
DIN = 9216
C_U, C_V, C_ZA, C_Q, C_K, C_VV, C_ZB, C_GA, C_GB = [i * 1024 for i in range(9)]


class Gen:
    def __init__(self, nc, stack):
        self.nc = nc
        self.stack = stack
        self.eng = dict(pe=nc.tensor, act=nc.scalar, dve=nc.vector, pool=nc.gpsimd, sp=nc.sync)
        self.sem = {}
        self.cnt = {}
        self.waited = {}
        for e in ["pe", "act", "dve", "pool"]:
            self.newsem(e)

    def newsem(self, name):
        self.sem[name] = self.stack.enter_context(self.nc.semaphore("s_" + name))
        self.cnt[name] = 0

    def wait(self, e, *toks):
        for t in toks:
            if t is None:
                continue
            if isinstance(t, list):
                self.wait(e, *t)
                continue
            name, val = t
            if val <= 0:
                continue
            k = (e, name)
            if self.waited.get(k, 0) >= val:
                continue
            self.waited[k] = val
            self.eng[e].wait_ge(self.sem[name], val)

    def mark(self, e, inst):
        self.cnt[e] += 1
        inst.then_inc(self.sem[e], 1)
        return (e, self.cnt[e])

    def last(self, e):
        return (e, self.cnt[e])

    def dma(self, e, out, in_, semname):
        inst = self.eng[e].dma_start(out=out, in_=in_)
        self.cnt[semname] += 16
        inst.then_inc(self.sem[semname], 16)
        return (semname, self.cnt[semname])

    def barrier(self):
        toks = [self.last(e) for e in ["pe", "act", "dve", "pool"]]
        for e in ["pe", "act", "dve", "pool", "sp"]:
            self.wait(e, *toks)


class _Stop(Exception):
    pass


STOP = [None]
OPT = dict(trdelay=7, g2late=True, mlate=True)


def _chk(stage):
    if STOP[0] == stage:
        raise _Stop()


def build_program(nb):
    nc = bass.Bass("TRN2", target_bir_lowering=False)
    x_d = nc.dram_tensor("x", [nb, S, D], F32, kind="ExternalInput").ap()
    w_in_d = nc.dram_tensor("w_in", [D, DIN], F32, kind="ExternalInput").ap()
    w_a_d = nc.dram_tensor("w_a", [D, D], F32, kind="ExternalInput").ap()
    w_b_d = nc.dram_tensor("w_b", [D, D], F32, kind="ExternalInput").ap()
    w_o_d = nc.dram_tensor("w_o", [D, D], F32, kind="ExternalInput").ap()
    gpreT_d = nc.dram_tensor("gpreT", [128, 8], F32, kind="ExternalInput").ap()
    gmgT_d = nc.dram_tensor("gmgT", [128, 8], F32, kind="ExternalInput").ap()
    gmb_d = nc.dram_tensor("gmb_row", [1, 1024], F32, kind="ExternalInput").ap()
    wsT_d = nc.dram_tensor("wsT", [128, 1024], F32, kind="ExternalInput").ap()
    bs_d = nc.dram_tensor("bs_row", [1, 1024], F32, kind="ExternalInput").ap()
    lams_d = nc.dram_tensor("lams", [1, 256], F32, kind="ExternalInput").ap()
    subg_d = nc.dram_tensor("subg_row", [1, 128], F32, kind="ExternalInput").ap()
    gpost_d = nc.dram_tensor("gpost_row", [1, 1024], F32, kind="ExternalInput").ap()
    ident_d = nc.dram_tensor("ident", [128, 128], F32, kind="ExternalInput").ap()
    rotT_d = nc.dram_tensor("rotT", [128, 128], F32, kind="ExternalInput").ap()
    cos_d = nc.dram_tensor("cosT", [128, S], F32, kind="ExternalInput").ap()
    sin_d = nc.dram_tensor("sinT", [128, S], F32, kind="ExternalInput").ap()
    out_d = nc.dram_tensor("out", [nb, S, D], F32, kind="ExternalOutput").ap()

    w_in_v = w_in_d.rearrange("(dc p) c -> p dc c", p=128)
    w_a_v = w_a_d.rearrange("(dc p) c -> p dc c", p=128)
    w_b_v = w_b_d.rearrange("(dc p) c -> p dc c", p=128)
    w_o_v = w_o_d.rearrange("(dc p) c -> p dc c", p=128)

    A = nc.alloc_sbuf_tensor
    hT = A("hT", [128, 8, S], BF16)
    ybT = A("ybT", [128, 8, S], BF16)
    Wt = [A("W0", [128, 8, 1024], BF16), A("W1", [128, 8, 1024], BF16)]
    cosT = A("cos", [128, S], F32)
    sinT = A("sin", [128, S], F32)
    xin = [A("xin0", [128, 1024], F32), A("xin1", [128, 1024], F32)]
    ybuf = [A("yb0", [128, 1024], F32), A("yb1", [128, 1024], F32)]
    junk = A("junk", [128, 1024], BF16)
    gpost_bc = A("gpost_bc", [128, 1024], F32)
    Cq = A("Cq", [128, 8, 128], F32)
    wsT = A("wsT_bf", [128, 8, 128], BF16)
    ident = A("ident_bf", [128, 128], BF16)
    rotT = A("rotT_bf", [128, 128], BF16)
    gsub_bc = A("gsub_bc", [128, 128], F32)
    gpreT = A("gpreT_sb", [128, 8], F32)
    gq = A("gq", [128, 8], F32)
    neghalf = A("neghalf", [128, 4], F32)
    neg_lam = A("neg_lam", [128, 1], F32)
    lamtmp = A("lamtmp", [128, 8], F32)
    onesrow = A("onesrow", [1, 128], F32)
    NS = nb * NTT
    st_ss = A("st_ss", [128, NS], F32)
    st_ms = A("st_ms", [128, NS], F32)
    st_rs = A("st_rs", [128, NS], F32)
    lv_sum = A("lv_sum", [128, NS], F32)
    lv_sq = A("lv_sq", [128, NS], F32)
    lv_mean = A("lv_mean", [128, NS], F32)
    lv_var = A("lv_var", [128, NS], F32)
    lv_rs = A("lv_rs", [128, NS], F32)
    o_ss = A("o_ss", [128, NS], F32)
    o_ms = A("o_ms", [128, NS], F32)
    o_rs = A("o_rs", [128, NS], F32)
    rd_t = [A(f"rd{i}", [128, 2, 2], F32) for i in range(2)]
    s2_t = [A(f"s2{i}", [128, 2], F32) for i in range(2)]
    t_sb = [A(f"tsb{i}", [128, 2, 128], F32) for i in range(2)]
    u2_sb = [A(f"u2sb{i}", [128, 2, 128], F32) for i in range(2)]
    o_sb = [A(f"osb{i}", [128, 2, 128], F32) for i in range(2)]
    sq_sb = [A(f"sqsb{i}", [128, 2, 128], F32) for i in range(2)]
    ssq_t = [A(f"ssq{i}", [128, 2], F32) for i in range(2)]
    ms_t = [A(f"mst{i}", [128, 2], F32) for i in range(2)]
    rs_t = [A(f"rst{i}", [128, 2], F32) for i in range(2)]
    yb_sb = [A(f"ybsb{i}", [128, 128], BF16) for i in range(8)]
    th_a2 = [A(f"tha2{i}", [128, 256], F32) for i in range(2)]
    RN = 25600
    R = A("R", [128, RN], BF16)
    kq = R[:, 0:8192].rearrange("p (a b t) -> p a b t", a=2, b=2)
    v_aug = R[:, 8192:8192 + 4160].rearrange("p (t h e) -> p t h e", t=16, h=2)
    o1 = 8192 + 4160
    gate = R[:, o1:o1 + 4096].rearrange("p (t e) -> p t e", t=16)
    o2 = o1 + 4096
    E = [R[:, o2 + i * 1024: o2 + (i + 1) * 1024] for i in range(3)]
    o3 = o2 + 3072
    raw = [R[:, o3 + i * 512: o3 + (i + 1) * 512] for i in range(2)]
    o4 = o3 + 1024
    t12 = [[R[:, o4 + (i * 2 + j) * 1024: o4 + (i * 2 + j + 1) * 1024].bitcast(F32) for j in range(2)]
           for i in range(2)]
    assert o4 + 4096 <= RN
    xn = [R[:, 16384 + i * 1024: 16384 + (i + 1) * 1024] for i in range(2)]
    yaT = R[:, 0:8192].rearrange("p (g t) -> p g t", g=8)
    vhat = R[:, 8192:16384].rearrange("p (t e) -> p t e", t=8)
    merged = R[:, 8192:16384].rearrange("p (g t) -> p g t", g=8)
    scr = [[R[:, 16384 + (i * 4 + j) * 1024: 16384 + (i * 4 + j + 1) * 1024].bitcast(F32) for j in range(4)]
           for i in range(2)]
    assert 16384 + 8192 <= RN
    b_bc = R[:, 0:2048].bitcast(F32)
    wsT_f = R[:, 2048:4096].bitcast(F32)
    bs_sb = R[0:1, 4096:6144].bitcast(F32)
    lq_bc = R[:, 6144:6656].bitcast(F32)
    subg_bc = R[:, 6656:6912].bitcast(F32)
    gmgT_sb = R[:, 6912:6928].bitcast(F32)
    lprod = R[:, 7168:7424].bitcast(F32)

    PS = [nc.alloc_psum_tensor(f"ps{i}", [128, 1024], F32) for i in range(4)]
    PSB = [p.bitcast(BF16) for p in PS]

    def bank(i):
        return PS[i // 2][:, (i % 2) * 512:(i % 2 + 1) * 512]

    def bank_bf(i):
        return PSB[i // 2][:, (i % 2) * 1024:(i % 2 + 1) * 1024]

    with ExitStack() as stack:
        g = Gen(nc, stack)
        for nm in ["w0", "w1", "xin0", "xin1", "out0", "out1", "cst", "cstp"]:
            g.newsem(nm)
        pe, act, dve, pool, sp = nc.tensor, nc.scalar, nc.vector, nc.gpsimd, nc.sync
        bank_free = [None] * 8

        def acquire(e, banks):
            g.wait(e, *[bank_free[b] for b in banks])

        def release(banks, tok):
            for b in banks:
                bank_free[b] = tok

        c = []
        c.append(g.dma("sp", cosT[:], cos_d, "cst"))
        c.append(g.dma("sp", sinT[:], sin_d, "cst"))
        c.append(g.dma("sp", gpost_bc[:], gpost_d.partition_broadcast(128), "cst"))
        c.append(g.dma("sp", gpreT[:], gpreT_d, "cst"))
        c.append(g.dma("sp", b_bc, gmb_d.partition_broadcast(128), "cst"))
        c.append(g.dma("sp", wsT_f, wsT_d, "cst"))
        c.append(g.dma("sp", bs_sb, bs_d, "cst"))
        c.append(g.dma("sp", lq_bc, lams_d.partition_broadcast(128), "cst"))
        c.append(g.dma("sp", subg_bc, subg_d.partition_broadcast(128), "cst"))
        c.append(g.dma("sp", gmgT_sb, gmgT_d, "cst"))
        cst_tok = c[-1]
        g.dma("pool", wsT[:].rearrange("p g i -> p (g i)"), wsT_d, "cstp")
        g.dma("pool", ident[:], ident_d, "cstp")
        cstp_tok = g.dma("pool", rotT[:], rotT_d, "cstp")
        for e in ["pe", "act", "dve", "pool"]:
            g.wait(e, cst_tok, cstp_tok)
        pool.memset(neghalf[:], -0.5)
        pool.memset(onesrow[:], 1.0)
        for t in [st_ss, lv_sum, lv_sq, o_ss]:
            pool.memset(t[:], 0.0)
        tk_pool0 = g.mark("pool", pool.memset(junk[:], 0.0))
        dve.tensor_tensor(out=lprod[:, 0:64], in0=lq_bc[:, 0:64], in1=lq_bc[:, 64:128], op=ALU.mult)
        tk = g.mark("dve", dve.tensor_tensor(out=lprod[:, 64:128], in0=lq_bc[:, 128:192],
                                             in1=lq_bc[:, 192:256], op=ALU.mult))
        g.wait("dve", tk)
        dve.tensor_reduce(out=lamtmp[:, 0:1], in_=lprod[:, 0:64], axis=AX.X, op=ALU.add)
        tk = g.mark("dve", dve.tensor_reduce(out=lamtmp[:, 1:2], in_=lprod[:, 64:128], axis=AX.X, op=ALU.add))
        g.wait("act", tk)
        tk = g.mark("act", act.activation(out=lamtmp[:, 2:4], in_=lamtmp[:, 0:2], func=AF.Exp))
        g.wait("dve", tk)
        tk = g.mark("dve", dve.tensor_tensor(out=lamtmp[:, 4:5], in0=lamtmp[:, 3:4], in1=lamtmp[:, 2:3],
                                             op=ALU.subtract))
        g.wait("dve", tk)
        tk = g.mark("dve", dve.tensor_scalar(out=neg_lam[:], in0=lamtmp[:, 4:5], scalar1=-LAM_INIT,
                                             scalar2=None, op0=ALU.add))
        dve.tensor_scalar(out=gsub_bc[:], in0=subg_bc, scalar1=(1.0 - LAM_INIT) * 0.25, scalar2=None,
                          op0=ALU.mult)
        tk = g.mark("dve", dve.tensor_scalar(out=gq[:], in0=gmgT_sb, scalar1=0.25, scalar2=None, op0=ALU.mult))
        g.wait("pe", tk_pool0)
        for gg in range(8):
            o = PS[gg // 4][:, (gg % 4) * 128:(gg % 4 + 1) * 128]
            pe.matmul(o, lhsT=b_bc[:, gg * 128:(gg + 1) * 128], rhs=wsT_f[:, gg * 128:(gg + 1) * 128],
                      start=True, stop=False)
            ins = pe.matmul(o, lhsT=onesrow[0:1, :], rhs=bs_sb[0:1, gg * 128:(gg + 1) * 128],
                            start=False, stop=True)
        tk = g.mark("pe", ins)
        g.wait("dve", tk)
        dve.tensor_scalar(out=Cq[:, 0:4, :].rearrange("p g i -> p (g i)"), in0=PS[0][:, 0:512], scalar1=0.25,
                          scalar2=None, op0=ALU.mult)
        tk = g.mark("dve", dve.tensor_scalar(out=Cq[:, 4:8, :].rearrange("p g i -> p (g i)"), in0=PS[1][:, 0:512],
                                             scalar1=0.25, scalar2=None, op0=ALU.mult))
        g.barrier()

        wstate = dict(idx=0, free=[None, None], ready={})

        def wblock_dmas(kind, arg):
            if kind == "att":
                hp = arg
                return [(j * 256, 256, w_in_v, base + hp * 256) for j, base in enumerate([C_Q, C_K, C_VV, C_ZB])]
            if kind == "g1":
                return [(0, 1024, w_in_v, C_V)]
            if kind == "g2":
                gqi = arg
                return [(0, 512, w_in_v, C_U + gqi * 512), (512, 512, w_in_v, C_ZA + gqi * 512)]
            if kind == "m":
                dp = arg
                return [(0, 256, w_a_v, dp * 256), (256, 256, w_b_v, dp * 256),
                        (512, 256, w_in_v, C_GA + dp * 256), (768, 256, w_in_v, C_GB + dp * 256)]
            if kind == "o":
                return [(0, 1024, w_o_v, 0)]
            raise ValueError(kind)

        blocks = []
        for b in range(nb):
            for hp in range(4):
                blocks.append(("att", hp))
            for hs in range(2):
                blocks.append(("g1", hs))
                blocks.append(("g2", 0))
                blocks.append(("g2", 1))
                for dp in range(4):
                    blocks.append(("m", dp))
                blocks.append(("o", hs))
        wload_i = [0]

        def wload_next():
            i = wload_i[0]
            if i >= len(blocks):
                return
            wload_i[0] += 1
            slot = i % 2
            g.wait("pool", wstate["free"][slot])
            kind, arg = blocks[i]
            tok = None
            for (c0, ncol, src, s0) in wblock_dmas(kind, arg):
                tok = g.dma("pool", Wt[slot][:, :, c0:c0 + ncol], src[:, :, s0:s0 + ncol], "w%d" % slot)
            wstate["ready"][i] = tok

        wuse_i = [0]

        def wuse(kind):
            i = wuse_i[0]
            wuse_i[0] += 1
            assert blocks[i][0] == kind, (blocks[i], kind)
            slot = i % 2
            return Wt[slot], wstate["ready"][i], slot

        def wdone(slot, tok):
            wstate["free"][slot] = tok
            wload_next()

        wload_next()
        wload_next()

        junk_tok = [None]
        seq_end = [None]
        hT_war = [None]
        ybT_war = [None]
        xin_free = [None, None]
        ybuf_free = [None, None]
        out_toks = []

        def phase_H(b):
            xn_free = [None, None]
            pend_load = {}

            def load(tt):
                s = tt % 2
                g.wait("sp", xin_free[s])
                pend_load[tt] = g.dma("sp", xin[s][:], x_d[b, tt * 128:(tt + 1) * 128, :], "xin%d" % s)

            def stage1(tt):
                s = tt % 2
                col = b * NTT + tt
                g.wait("act", pend_load[tt], junk_tok[0])
                tA = g.mark("act", act.activation(out=junk[:], in_=xin[s][:], func=AF.Square,
                                                  accum_out=st_ss[:, col:col + 1]))
                junk_tok[0] = tA
                g.wait("dve", tA)
                tD1 = g.mark("dve", dve.tensor_scalar(out=st_ms[:, col:col + 1], in0=st_ss[:, col:col + 1],
                                                      scalar1=1.0 / D, scalar2=EPS, op0=ALU.mult, op1=ALU.add))
                g.wait("pool", tD1)
                tP = g.mark("pool", pool.tensor_tensor(out=st_rs[:, col:col + 1], in0=st_ms[:, col:col + 1],
                                                       in1=neghalf[:, 0:1], op=ALU.pow))
                g.wait("dve", tP, xn_free[s], seq_end[0])
                tD2 = g.mark("dve", dve.tensor_scalar(out=xn[s], in0=xin[s][:], scalar1=st_rs[:, col:col + 1],
                                                      scalar2=None, op0=ALU.mult))
                xin_free[s] = tD2
                pb = 2 + (tt % 2)
                acquire("pe", [2 * pb, 2 * pb + 1])
                g.wait("pe", tD2)
                for dc in range(DC):
                    ins = pe.matmul(PS[pb][:, dc * 128:(dc + 1) * 128], lhsT=xn[s][:, dc * 128:(dc + 1) * 128],
                                    rhs=ident[:], start=True, stop=True)
                tT = g.mark("pe", ins)
                xn_free[s] = tT
                return tT, pb

            def stage2(tt, tT, pb):
                g.wait("dve", tT, hT_war[0])
                tD3 = g.mark("dve", dve.tensor_tensor(
                    out=hT[:, :, tt * 128:(tt + 1) * 128],
                    in0=PS[pb][:].rearrange("p (c t) -> p c t", c=8),
                    in1=gpreT[:, :].unsqueeze(2).broadcast_to([128, 8, 128]), op=ALU.mult))
                release([2 * pb, 2 * pb + 1], tD3)

            load(0)
            prev = None
            for tt in range(NTT):
                if tt + 1 < NTT:
                    load(tt + 1)
                cur = stage1(tt)
                if prev is not None:
                    stage2(tt - 1, *prev)
                prev = cur
            stage2(NTT - 1, *prev)
            return g.last("dve")

        def phase_A(b, hT_ready):
            for e_ in ["pe", "act", "dve", "pool"]:
                g.wait(e_, seq_end[0])
            g.wait("pool", g.last("pe"))
            tk_ones = g.mark("pool", pool.memset(v_aug[:, :, :, 128:130], 1.0))
            kq_war = [None]
            vaug_war = [None]
            gate_war = [None]
            E_free = [None, None, None]
            raw_free = [None, None]
            t12_free = [None, None]
            th_free = [None, None]
            post_free = [None, None]
            yb_free = [None] * 8
            ectr = [0]
            sctr = [0]
            uctr = [0]
            gstep = [0]
            pend_tr = []
            for hp in range(4):
                Wb, wtok, wslot = wuse("att")
                g.wait("pe", wtok, hT_ready)
                items = [(which, hh, tc) for which in range(2) for hh in range(2) for tc in range(4)]
                pairs = [(6, 7), (0, 1), (2, 3)]
                pend = None
                kq_ready = None

                def a1_second(it, i, bx, by, tX):
                    which, hh, tc = it
                    r = i % 2
                    g.wait("act", tX, raw_free[r])
                    tA = g.mark("act", act.activation(out=raw[r], in_=bank(bx), func=AF.Copy))
                    acquire("pe", [by])
                    g.wait("pe", tA)
                    tY = g.mark("pe", pe.matmul(bank(by), lhsT=rotT[:], rhs=raw[r], start=True, stop=True))
                    raw_free[r] = tY
                    g.wait("dve", tY, t12_free[r])
                    dve.tensor_tensor(out=t12[r][0], in0=bank(bx), in1=cosT[:, tc * 512:(tc + 1) * 512], op=ALU.mult)
                    tD = g.mark("dve", dve.tensor_tensor(out=t12[r][1], in0=bank(by),
                                                         in1=sinT[:, tc * 512:(tc + 1) * 512], op=ALU.mult))
                    release([bx, by], tD)
                    g.wait("pool", tD, kq_war[0])
                    tP = g.mark("pool", pool.tensor_tensor(out=kq[:, which, hh, tc * 512:(tc + 1) * 512],
                                                           in0=t12[r][0], in1=t12[r][1], op=ALU.add))
                    t12_free[r] = tP
                    return tP

                for i, it in enumerate(items):
                    which, hh, tc = it
                    bx, by = pairs[i % 3]
                    acquire("pe", [bx])
                    wc = (1 - which) * 256 + hh * 128
                    for dc in range(DC):
                        ins = pe.matmul(bank(bx), lhsT=Wb[:, dc, wc:wc + 128], rhs=hT[:, dc, tc * 512:(tc + 1) * 512],
                                        start=(dc == 0), stop=(dc == DC - 1))
                    tX = g.mark("pe", ins)
                    if pend is not None:
                        kq_ready = a1_second(*pend)
                    pend = (it, i, bx, by, tX)
                kq_ready = a1_second(*pend)
                _chk("A1")
                a2banks = [6, 7, 0, 1, 2, 3]
                for tt in range(NTT):
                    bk = a2banks[tt % 6]
                    acquire("pe", [bk])
                    for dc in range(DC):
                        ins = pe.matmul(bank(bk), lhsT=hT[:, dc, tt * 128:(tt + 1) * 128], rhs=Wb[:, dc, 512:1024],
                                        start=(dc == 0), stop=(dc == DC - 1))
                    tX = g.mark("pe", ins)
                    r = tt % 2
                    g.wait("act", tX, th_free[r])
                    tA = g.mark("act", act.activation(out=th_a2[r][:], in_=bank(bk)[:, 256:512], func=AF.Tanh, scale=0.5))
                    g.wait("dve", tA, vaug_war[0], gate_war[0], tk_ones)
                    dve.tensor_copy(out=v_aug[:, tt, :, 0:128],
                                    in_=bank(bk)[:, 0:256].rearrange("p (h e) -> p h e", h=2))
                    tD = g.mark("dve", dve.scalar_tensor_tensor(out=gate[:, tt, :], in0=th_a2[r][:], scalar=1.0,
                                                                in1=bank(bk)[:, 256:512], op0=ALU.add, op1=ALU.mult))
                    th_free[r] = tD
                    release([bk], tD)
                a2_ready = g.last("dve")
                wdone(wslot, g.last("pe"))
                _chk("A2")
                steps = [(hh, cq, st) for hh in range(2) for cq in range(16) for st in range(4)]
                sbanks = [(0, 1), (2, 3)]
                qk_tok = {}

                def emit_qk(si):
                    hh, cq, st = steps[si]
                    sb = sctr[0] % 2
                    sctr[0] += 1
                    acquire("pe", sbanks[sb])
                    g.wait("pe", kq_ready)
                    Sb = PS[sb]
                    for kk in range(4):
                        kt = 4 * st + kk
                        for m in range(2):
                            ins = pe.matmul(Sb[:, (m * 4 + kk) * 128:(m * 4 + kk + 1) * 128],
                                            lhsT=kq[64 * m:64 * m + 64, 0, hh, kt * 128:(kt + 1) * 128],
                                            rhs=kq[64 * m:64 * m + 64, 1, hh, cq * 128:(cq + 1) * 128],
                                            start=True, stop=True)
                    tQ = g.mark("pe", ins)
                    eb = ectr[0] % 3
                    ectr[0] += 1
                    g.wait("act", tQ, E_free[eb])
                    tE = g.mark("act", act.activation(out=E[eb], in_=Sb[:], func=AF.Exp, scale=0.125))
                    release(sbanks[sb], tE)
                    qk_tok[si] = (tE, eb)

                def emit_pv(si):
                    hh, cq, st = steps[si]
                    tE, eb = qk_tok.pop(si)
                    g.wait("pe", tE, a2_ready)
                    u = uctr[0]
                    ob = 4 + (u % 2)
                    if st == 0:
                        acquire("pe", [ob])
                    Ov = bank(ob).rearrange("p (m c) -> p m c", m=2)
                    for kk in range(4):
                        kt = 4 * st + kk
                        for m in range(2):
                            c0 = (m * 4 + kk) * 128
                            ins = pe.matmul(Ov[:, m, 0:129], lhsT=E[eb][:, c0:c0 + 128],
                                            rhs=v_aug[:, kt, hh, 0:129],
                                            start=(st == 0 and kk == 0 and m == 0),
                                            stop=(st == 3 and kk == 3), skip_group_check=True)
                    tP = g.mark("pe", ins)
                    E_free[eb] = tP
                    if st == 3:
                        uctr[0] += 1
                        post(hh, cq, tP, u, ob, si)

                def post(hh, cq, tPV, u, ob, si):
                    s = u % 2
                    Ov = bank(ob).rearrange("p (m c) -> p m c", m=2)
                    rd = rd_t[s][:, 0, :]
                    g.wait("dve", tPV)
                    t0 = g.mark("dve", dve.reciprocal(out=rd, in_=Ov[:, :, 128]))
                    g.wait("dve", t0)
                    dve.tensor_scalar(out=t_sb[s][:, 0, :], in0=Ov[:, 0, 0:128], scalar1=rd[:, 0:1], scalar2=None,
                                      op0=ALU.mult)
                    t1 = g.mark("dve", dve.tensor_scalar(out=s2_t[s][:, 0:1], in0=rd[:, 1:2], scalar1=neg_lam[:, 0:1],
                                                         scalar2=None, op0=ALU.mult))
                    g.wait("dve", t1)
                    tD = g.mark("dve", dve.scalar_tensor_tensor(out=o_sb[s][:, 0, :], in0=Ov[:, 1, 0:128],
                                                                scalar=s2_t[s][:, 0:1], in1=t_sb[s][:, 0, :],
                                                                op0=ALU.mult, op1=ALU.add))
                    release([ob], tD)
                    g.wait("dve", tD)
                    td = g.mark("dve", dve.scalar_tensor_tensor(out=sq_sb[s][:, 0, :], in0=o_sb[s][:, 0, :], scalar=1.0,
                                                                in1=o_sb[s][:, 0, :], op0=ALU.mult, op1=ALU.mult,
                                                                accum_out=ssq_t[s][:, 0:1]))
                    g.wait("dve", td)
                    td = g.mark("dve", dve.tensor_scalar(out=ms_t[s][:, 0:1], in0=ssq_t[s][:, 0:1], scalar1=1.0 / 128,
                                                         scalar2=EPS, op0=ALU.mult, op1=ALU.add))
                    g.wait("pool", td)
                    tp = g.mark("pool", pool.tensor_tensor(out=rs_t[s][:, 0:1], in0=ms_t[s][:, 0:1], in1=neghalf[:, 0:1],
                                                           op=ALU.pow))
                    g.wait("dve", tp)
                    td = g.mark("dve", dve.scalar_tensor_tensor(out=sq_sb[s][:, 1, :], in0=o_sb[s][:, 0, :],
                                                                scalar=rs_t[s][:, 0:1], in1=gsub_bc[:, :],
                                                                op0=ALU.mult, op1=ALU.mult))
                    ys = u % 8
                    g.wait("dve", td, yb_free[ys])
                    tY = g.mark("dve", dve.tensor_tensor(out=yb_sb[ys][:, :], in0=sq_sb[s][:, 1, :],
                                                         in1=gate[:, cq, hh * 128:(hh + 1) * 128], op=ALU.mult))
                    pend_tr.append((hp * 2 + hh, cq, ys, tY, gstep[0]))

                def emit_tr():
                    (h, cq0, ys0, tY0, _), (h1, cq1, ys1, tY1, _) = pend_tr.pop(0), pend_tr.pop(0)
                    assert h == h1 and cq1 == cq0 + 1
                    bk = 6 + ((cq0 // 2) % 2)
                    acquire("pe", [bk])
                    g.wait("pe", tY0, tY1)
                    pe.transpose(bank_bf(bk)[:, 0:128], yb_sb[ys0][:, :], ident[:])
                    tT = g.mark("pe", pe.transpose(bank_bf(bk)[:, 128:256], yb_sb[ys1][:, :], ident[:]))
                    yb_free[ys0] = tT
                    yb_free[ys1] = tT
                    g.wait("dve", tT, ybT_war[0])
                    tD = g.mark("dve", dve.tensor_copy(out=ybT[:, h, cq0 * 128:(cq0 + 2) * 128], in_=bank_bf(bk)[:, 0:256]))
                    release([bk], tD)

                n = len(steps)
                emit_qk(0)
                emit_qk(1)
                for i in range(n):
                    if i + 2 < n:
                        emit_qk(i + 2)
                    emit_pv(i)
                    gstep[0] += 1
                    if len(pend_tr) >= 2 and gstep[0] >= pend_tr[1][4] + OPT['trdelay']:
                        emit_tr()
                if hp == 3:
                    while len(pend_tr) >= 2:
                        emit_tr()
                    assert not pend_tr
                tk_att = g.last("pe")
                kq_war[0] = tk_att
                vaug_war[0] = tk_att
                gate_war[0] = g.last("dve")

        def phase_G(b, hs):
            tok0 = hs * 1024
            Wb, wtok, wslot = wuse("g1")
            g.wait("pe", wtok)
            for tt in range(8):
                gt = hs * 8 + tt
                col = b * NTT + gt
                pb = tt % 4
                acquire("pe", [2 * pb, 2 * pb + 1])
                for half in range(2):
                    for dc in range(DC):
                        ins = pe.matmul(PS[pb][:, half * 512:(half + 1) * 512], lhsT=hT[:, dc, gt * 128:(gt + 1) * 128],
                                        rhs=Wb[:, dc, half * 512:(half + 1) * 512], start=(dc == 0), stop=(dc == DC - 1))
                tX = g.mark("pe", ins)
                g.wait("act", tX, junk_tok[0])
                tA0 = g.mark("act", act.activation(out=junk[:], in_=PS[pb][:], func=AF.Identity,
                                                   accum_out=lv_sum[:, col:col + 1]))
                g.wait("act", tA0)
                tA = g.mark("act", act.activation(out=junk[:], in_=PS[pb][:], func=AF.Square,
                                                  accum_out=lv_sq[:, col:col + 1]))
                junk_tok[0] = tA
                g.wait("dve", tA)
                td = g.mark("dve", dve.tensor_scalar(out=lv_mean[:, col:col + 1], in0=lv_sum[:, col:col + 1],
                                                     scalar1=1.0 / D, scalar2=None, op0=ALU.mult))
                g.wait("dve", td)
                td = g.mark("dve", dve.scalar_tensor_tensor(out=lv_var[:, col:col + 1], in0=lv_mean[:, col:col + 1],
                                                            scalar=-1.0, in1=lv_mean[:, col:col + 1],
                                                            op0=ALU.mult, op1=ALU.mult))
                g.wait("dve", td)
                td = g.mark("dve", dve.scalar_tensor_tensor(out=lv_var[:, col:col + 1], in0=lv_sq[:, col:col + 1],
                                                            scalar=1.0 / D, in1=lv_var[:, col:col + 1],
                                                            op0=ALU.mult, op1=ALU.add))
                g.wait("dve", td)
                td = g.mark("dve", dve.tensor_scalar(out=lv_var[:, col:col + 1], in0=lv_var[:, col:col + 1],
                                                     scalar1=EPS, scalar2=None, op0=ALU.add))
                g.wait("pool", td)
                tp = g.mark("pool", pool.tensor_tensor(out=lv_rs[:, col:col + 1], in0=lv_var[:, col:col + 1],
                                                       in1=neghalf[:, 0:1], op=ALU.pow))
                g.wait("dve", tp)
                tD = g.mark("dve", dve.tensor_scalar(out=vhat[:, tt, :], in0=PS[pb][:], scalar1=lv_mean[:, col:col + 1],
                                                     scalar2=lv_rs[:, col:col + 1], op0=ALU.subtract, op1=ALU.mult))
                release([2 * pb, 2 * pb + 1], tD)
            vhat_ready = g.last("dve")
            wdone(wslot, g.last("pe"))
            _chk("G1")
            scr_free = [None, None]
            ictr = 0
            for gqi in range(2):
                Wb, wtok, wslot = wuse("g2")
                g.wait("pe", wtok)
                if not OPT['g2late']:
                    g.wait("pe", vhat_ready)
                for g4 in range(4):
                    gh = gqi * 4 + g4
                    for tc in range(2):
                        si = ictr % 2
                        ictr += 1
                        bu, bz, bv = (0, 1, 2) if si == 0 else (3, 4, 5)
                        acquire("pe", [bu, bz, bv])
                        t0 = tok0 + tc * 512
                        for dc in range(DC):
                            pe.matmul(bank(bu), lhsT=Wb[:, dc, g4 * 128:(g4 + 1) * 128], rhs=hT[:, dc, t0:t0 + 512],
                                      start=(dc == 0), stop=(dc == DC - 1))
                        for dc in range(DC):
                            pe.matmul(bank(bz), lhsT=Wb[:, dc, 512 + g4 * 128:512 + (g4 + 1) * 128],
                                      rhs=hT[:, dc, t0:t0 + 512], start=(dc == 0), stop=(dc == DC - 1))
                        g.wait("pe", vhat_ready)
                        for cc in range(4):
                            ins = pe.matmul(bank(bv)[:, cc * 128:(cc + 1) * 128],
                                            lhsT=vhat[:, tc * 4 + cc, gh * 128:(gh + 1) * 128], rhs=wsT[:, gh, :],
                                            start=True, stop=True)
                        tX = g.mark("pe", ins)
                        th, sg, tt_, svp = scr[si]
                        g.wait("act", tX, scr_free[si])
                        tA = g.mark("act", act.activation(out=th, in_=bank(bz), func=AF.Tanh, scale=0.5))
                        g.wait("dve", tA)
                        td = g.mark("dve", dve.scalar_tensor_tensor(out=sg, in0=th, scalar=1.0, in1=bank(bz),
                                                                    op0=ALU.add, op1=ALU.mult))
                        g.wait("dve", td)
                        dve.tensor_tensor(out=tt_, in0=bank(bu), in1=sg, op=ALU.mult)
                        tD = g.mark("dve", dve.scalar_tensor_tensor(
                            out=svp.rearrange("p (c i) -> p c i", c=4),
                            in0=bank(bv).rearrange("p (c i) -> p c i", c=4), scalar=gq[:, gh:gh + 1],
                            in1=Cq[:, gh, :].unsqueeze(1).broadcast_to([128, 4, 128]),
                            op0=ALU.mult, op1=ALU.add))
                        release([bu, bz, bv], tD)
                        g.wait("pool", tD)
                        tP = g.mark("pool", pool.tensor_tensor(out=yaT[:, gh, tc * 512:(tc + 1) * 512], in0=svp, in1=tt_,
                                                               op=ALU.mult))
                        scr_free[si] = tP
                wdone(wslot, g.last("pe"))
            ya_ready = g.last("pool")
            _chk("G2")
            for dp in range(4):
                Wb, wtok, wslot = wuse("m")
                g.wait("pe", wtok)
                if not OPT['mlate']:
                    g.wait("pe", ya_ready)
                for dd in range(2):
                    dt_ = dp * 2 + dd
                    for tc in range(2):
                        si = ictr % 2
                        ictr += 1
                        bks = [6, 7, 0, 1] if si == 0 else [2, 3, 4, 5]
                        acquire("pe", bks)
                        t0 = tok0 + tc * 512
                        for dc in range(DC):
                            pe.matmul(bank(bks[1]), lhsT=Wb[:, dc, 256 + dd * 128:256 + (dd + 1) * 128],
                                      rhs=ybT[:, dc, t0:t0 + 512], start=(dc == 0), stop=(dc == DC - 1))
                        for dc in range(DC):
                            pe.matmul(bank(bks[2]), lhsT=Wb[:, dc, 512 + dd * 128:512 + (dd + 1) * 128],
                                      rhs=hT[:, dc, t0:t0 + 512], start=(dc == 0), stop=(dc == DC - 1))
                        for dc in range(DC):
                            pe.matmul(bank(bks[3]), lhsT=Wb[:, dc, 768 + dd * 128:768 + (dd + 1) * 128],
                                      rhs=hT[:, dc, t0:t0 + 512], start=(dc == 0), stop=(dc == DC - 1))
                        g.wait("pe", ya_ready)
                        for dc in range(DC):
                            ins = pe.matmul(bank(bks[0]), lhsT=Wb[:, dc, dd * 128:(dd + 1) * 128],
                                            rhs=yaT[:, dc, tc * 512:(tc + 1) * 512], start=(dc == 0), stop=(dc == DC - 1))
                        tX = g.mark("pe", ins)
                        tha, thb, m1, m2 = scr[si]
                        g.wait("act", tX, scr_free[si])
                        act.activation(out=tha, in_=bank(bks[2]), func=AF.Tanh, scale=0.5)
                        tA = g.mark("act", act.activation(out=thb, in_=bank(bks[3]), func=AF.Tanh, scale=0.5))
                        g.wait("dve", tA)
                        dve.scalar_tensor_tensor(out=m1, in0=tha, scalar=1.0, in1=bank(bks[0]), op0=ALU.add, op1=ALU.mult)
                        tD = g.mark("dve", dve.scalar_tensor_tensor(out=m2, in0=thb, scalar=1.0, in1=bank(bks[1]),
                                                                    op0=ALU.add, op1=ALU.mult))
                        release(bks, tD)
                        g.wait("pool", tD)
                        tP = g.mark("pool", pool.tensor_tensor(out=merged[:, dt_, tc * 512:(tc + 1) * 512], in0=m1, in1=m2,
                                                               op=ALU.add))
                        scr_free[si] = tP
                wdone(wslot, g.last("pe"))
            mg_ready = g.last("pool")
            _chk("M")
            Wb, wtok, wslot = wuse("o")
            g.wait("pe", wtok, mg_ready)
            pend_load = {}

            def loadx(tt):
                s = tt % 2
                gt = hs * 8 + tt
                g.wait("sp", xin_free[s])
                pend_load[tt] = g.dma("sp", xin[s][:], x_d[b, gt * 128:(gt + 1) * 128, :], "xin%d" % s)

            loadx(0)
            for tt in range(8):
                if tt + 1 < 8:
                    loadx(tt + 1)
                gt = hs * 8 + tt
                col = b * NTT + gt
                s = tt % 2
                pb = tt % 4
                acquire("pe", [2 * pb, 2 * pb + 1])
                for half in range(2):
                    for dc in range(DC):
                        ins = pe.matmul(PS[pb][:, half * 512:(half + 1) * 512], lhsT=merged[:, dc, tt * 128:(tt + 1) * 128],
                                        rhs=Wb[:, dc, half * 512:(half + 1) * 512], start=(dc == 0), stop=(dc == DC - 1))
                tX = g.mark("pe", ins)
                g.wait("act", tX, junk_tok[0])
                tA = g.mark("act", act.activation(out=junk[:], in_=PS[pb][:], func=AF.Square,
                                                  accum_out=o_ss[:, col:col + 1]))
                junk_tok[0] = tA
                g.wait("dve", tA)
                td = g.mark("dve", dve.tensor_scalar(out=o_ms[:, col:col + 1], in0=o_ss[:, col:col + 1],
                                                     scalar1=1.0 / D, scalar2=EPS, op0=ALU.mult, op1=ALU.add))
                g.wait("pool", td)
                tp = g.mark("pool", pool.tensor_tensor(out=o_rs[:, col:col + 1], in0=o_ms[:, col:col + 1],
                                                       in1=neghalf[:, 0:1], op=ALU.pow))
                g.wait("dve", tp, ybuf_free[s])
                tD = g.mark("dve", dve.scalar_tensor_tensor(out=ybuf[s][:], in0=PS[pb][:], scalar=o_rs[:, col:col + 1],
                                                            in1=gpost_bc[:], op0=ALU.mult, op1=ALU.mult))
                release([2 * pb, 2 * pb + 1], tD)
                g.wait("dve", tD, pend_load[tt])
                tP = g.mark("dve", dve.tensor_tensor(out=ybuf[s][:], in0=ybuf[s][:], in1=xin[s][:], op=ALU.add))
                xin_free[s] = tP
                g.wait("sp", tP)
                tO = g.dma("sp", out_d[b, gt * 128:(gt + 1) * 128, :], ybuf[s][:], "out%d" % s)
                ybuf_free[s] = tO
                out_toks.append(tO)
            wdone(wslot, g.last("pe"))

        try:
            _chk("setup")
            for b in range(nb):
                hT_ready = phase_H(b)
                _chk("H")
                phase_A(b, hT_ready)
                _chk("A")
                g.barrier()
                for hs in range(2):
                    phase_G(b, hs)
                hT_war[0] = g.last("pe")
                ybT_war[0] = g.last("pe")
                seq_end[0] = [g.last(e) for e in ["pe", "act", "dve", "pool"]]
        except _Stop:
            g.barrier()
        sp.wait_ge(g.sem["out0"], g.cnt["out0"])
        sp.wait_ge(g.sem["out1"], g.cnt["out1"])
    return nc


def _host_consts():
    inv_freq = (1.0 / (10000.0 ** (np.arange(0, 64, 2, dtype=np.float32) / np.float32(64)))).astype(np.float32)
    pos = np.arange(S, dtype=np.float32)
    ang = (pos[:, None] * inv_freq[None, :]).astype(np.float32)
    cos = np.cos(ang).astype(np.float32).T
    sin = np.sin(ang).astype(np.float32).T
    cosT = np.zeros((128, S), np.float32)
    sinT = np.zeros((128, S), np.float32)
    rotT = np.zeros((128, 128), np.float32)
    for p in range(128):
        i = p % 32
        cosT[p] = cos[i]
        if (p % 64) < 32:
            sinT[p] = -sin[i]
            rotT[p + 32, p] = 1.0
        else:
            sinT[p] = sin[i]
            rotT[p - 32, p] = 1.0
    ident = np.eye(128, dtype=np.float32)
    return cosT, sinT, rotT, ident


def make_in_maps(inputs, nb, ncores):
    f = lambda a: np.ascontiguousarray(np.asarray(a, dtype=np.float32))
    x = f(inputs["x"])
    cosT, sinT, rotT, ident = _host_consts()
    shared = {
        "w_in": f(inputs["w_in"][0]),
        "w_a": f(inputs["w_branch_a"][0]),
        "w_b": f(inputs["w_branch_b"][0]),
        "w_o": f(inputs["w_out"][0]),
        "gpreT": f(np.asarray(inputs["ln_pre_g"][0]).reshape(8, 128).T),
        "gmgT": f(np.asarray(inputs["gm_ln_g"][0]).reshape(8, 128).T),
        "gmb_row": f(np.asarray(inputs["gm_ln_b"][0]).reshape(1, 1024)),
        "wsT": f(np.transpose(np.asarray(inputs["gm_ws"][0]), (2, 0, 1)).reshape(128, 1024)),
        "bs_row": f(np.asarray(inputs["gm_bs"][0]).reshape(1, 1024)),
        "lams": f(np.concatenate([np.asarray(inputs[k][0]).reshape(-1) for k in
                                  ["lambda_q1", "lambda_k1", "lambda_q2", "lambda_k2"]]).reshape(1, 256)),
        "subg_row": f(np.asarray(inputs["da_subln_g"][0]).reshape(1, 128)),
        "gpost_row": f(np.asarray(inputs["ln_post_g"][0]).reshape(1, 1024)),
        "ident": ident, "rotT": rotT, "cosT": cosT, "sinT": sinT,
    }
    maps = []
    for c in range(ncores):
        m = dict(shared)
        m["x"] = np.ascontiguousarray(x[c * nb:(c + 1) * nb])
        maps.append(m)
    return maps


_NC_CACHE = {}


def kernel(**inputs):
    nb = inputs["x"].shape[0] // NCORES
    if nb not in _NC_CACHE:
        _NC_CACHE[nb] = build_program(nb)
    nc = _NC_CACHE[nb]
    maps = make_in_maps(inputs, nb, NCORES)
    res = run_bass_kernel_spmd(nc, maps, core_ids=list(range(NCORES)))
    out = np.concatenate([np.asarray(r["out"]) for r in res.results], axis=0)
    return out.astype(np.float32)
```
